# Optimizing a Trainium2 kernel written in Bass

```python
import math
import jax, jax.numpy as jnp
from jax import lax
import numpy as np

D_MODEL = 4096
BATCH = 1
SEQ = 16384
DEPTH = 4

GRID_W = 64
HEAD_DIM = 128
A_HEADS = 8
A_WIDTH = A_HEADS * 2 * HEAD_DIM
B_HEADS = 16
B_Q_LORA = 1536
B_KV_LORA = 512
B_NOPE = 128
B_ROPE = 64
B_V = 128
B_WIDTH = B_HEADS * B_V
C_HEADS = 16
C_WIDTH = C_HEADS * HEAD_DIM
NA_ROWS_MAX = 8
NA_COLS = 16

Q_BLOCK = 128
ROPE_BASE = 10000.0
EPS = 1e-6

IN_SPLITS = (A_WIDTH, A_WIDTH, A_WIDTH, A_WIDTH,
             B_Q_LORA, B_KV_LORA, B_ROPE, B_WIDTH,
             C_WIDTH, C_WIDTH, C_WIDTH, C_WIDTH,
             D_MODEL, D_MODEL, D_MODEL)
IN_COLS = sum(IN_SPLITS)
SPLIT_POINTS = tuple(int(v) for v in np.cumsum(IN_SPLITS)[:-1])

kernel_name = 'hybrid_diff_mla_neighbourhood_encoder'


def rmsnorm(x, g):
    xf = x.astype(jnp.float32)
    y = xf * lax.rsqrt(jnp.mean(xf * xf, axis=-1, keepdims=True) + EPS)
    return (y * g.astype(jnp.float32)).astype(x.dtype)


def alibi_slopes(n_heads):
    return jnp.asarray(2.0 ** (-8.0 * np.arange(1, n_heads + 1) / n_heads), dtype=jnp.float32)


def rope_tables(seq):
    inv_freq = ROPE_BASE ** (-jnp.arange(0, B_ROPE, 2, dtype=jnp.float32) / B_ROPE)
    ang = jnp.arange(seq, dtype=jnp.float32)[:, None] * inv_freq[None, :]
    return jnp.cos(ang), jnp.sin(ang)


def apply_rope(x, cos, sin):
    xf = x.astype(jnp.float32)
    x1, x2 = jnp.split(xf, 2, axis=-1)
    return jnp.concatenate([x1 * cos - x2 * sin, x2 * cos + x1 * sin], axis=-1).astype(x.dtype)


def to_blocks(t):
    b, s = t.shape[:2]
    t = t.reshape((b, s // Q_BLOCK, Q_BLOCK) + t.shape[2:])
    return jnp.moveaxis(t, 1, 0)


def from_blocks(t):
    t = jnp.moveaxis(t, 0, 1)
    return t.reshape((t.shape[0], t.shape[1] * t.shape[2]) + t.shape[3:])


def diff_attention(q, k, v, lam, lam_init, subln_g):
    s_len, d = q.shape[1], q.shape[-1]
    scale = d ** -0.5
    slopes = alibi_slopes(q.shape[2])
    pos = jnp.arange(s_len, dtype=jnp.float32)

    def block(args):
        qi, pi = args
        s = jnp.einsum('bqhmd,bkhmd->bmhqk', qi, k, preferred_element_type=jnp.float32) * scale
        dist = jnp.abs(pi[:, None] - pos[None, :])
        s = s - slopes[:, None, None] * dist[None]
        p = jax.nn.softmax(s, axis=-1)
        w = p[:, 0] - lam * p[:, 1]
        return jnp.einsum('bhqk,bkhe->bqhe', w.astype(v.dtype), v)

    o = from_blocks(lax.map(block, (to_blocks(q), pos.reshape(-1, Q_BLOCK))))
    o = rmsnorm(o, subln_g) * (1.0 - lam_init)
    return o.reshape(o.shape[:2] + (-1,))


def mla(c_q, c_kv, k_rope, q_norm_g, kv_norm_g, w_uq, w_ukv):
    b, s_len, _ = c_q.shape
    q = (rmsnorm(c_q, q_norm_g) @ w_uq).reshape(b, s_len, B_HEADS, B_NOPE + B_ROPE)
    q_nope, q_rope = q[..., :B_NOPE], q[..., B_NOPE:]
    kv = (rmsnorm(c_kv, kv_norm_g) @ w_ukv).reshape(b, s_len, B_HEADS, B_NOPE + B_V)
    k_nope, v = kv[..., :B_NOPE], kv[..., B_NOPE:]
    cos, sin = rope_tables(s_len)
    q_rope = apply_rope(q_rope, cos[:, None, :], sin[:, None, :])
    k_rope = apply_rope(k_rope, cos, sin)
    scale = (B_NOPE + B_ROPE) ** -0.5

    def block(args):
        qn, qr = args
        s = (jnp.einsum('bqhd,bkhd->bhqk', qn, k_nope, preferred_element_type=jnp.float32)
             + jnp.einsum('bqhr,bkr->bhqk', qr, k_rope, preferred_element_type=jnp.float32)) * scale
        p = jax.nn.softmax(s, axis=-1)
        return jnp.einsum('bhqk,bkhd->bqhd', p.astype(v.dtype), v)

    o = from_blocks(lax.map(block, (to_blocks(q_nope), to_blocks(q_rope))))
    return o.reshape(b, s_len, B_WIDTH)


def neighbourhood_attention(q, k, v, rpb):
    b, s_len, h, d = q.shape
    rows = s_len // GRID_W
    kr = min(NA_ROWS_MAX, rows)
    qg = q.reshape(b, rows, GRID_W, h, d)
    kg = k.reshape(b, rows, GRID_W, h, d)
    vg = v.reshape(b, rows, GRID_W, h, d)
    col = jnp.arange(GRID_W)
    col_idx = jnp.clip(col - NA_COLS // 2, 0, GRID_W - NA_COLS)[:, None] + jnp.arange(NA_COLS)[None, :]
    dc_idx = (col_idx - col[:, None] + NA_COLS - 1)[:, None, :]
    scale = d ** -0.5

    def row_fn(r):
        rs = jnp.clip(r - kr // 2, 0, rows - kr)
        k_nb = lax.dynamic_slice_in_dim(kg, rs, kr, axis=1)[:, :, col_idx]
        v_nb = lax.dynamic_slice_in_dim(vg, rs, kr, axis=1)[:, :, col_idx]
        q_row = lax.dynamic_index_in_dim(qg, r, axis=1, keepdims=False)
        s = jnp.einsum('bchd,brcwhd->bhcrw', q_row, k_nb, preferred_element_type=jnp.float32) * scale
        dr_idx = (rs + jnp.arange(kr) - r + NA_ROWS_MAX - 1)[None, :, None]
        s = s + rpb[:, dr_idx, dc_idx].astype(jnp.float32)[None]
        p = jax.nn.softmax(s.reshape(b, h, GRID_W, kr * NA_COLS), axis=-1).reshape(s.shape)
        return jnp.einsum('bhcrw,brcwhd->bchd', p.astype(v.dtype), v_nb)

    o = lax.map(row_fn, jnp.arange(rows))
    return jnp.moveaxis(o, 0, 1).reshape(b, s_len, h * d)


def setup_inputs(seed: int = 0) -> dict:
    key = jax.random.key(seed)
    ks = jax.random.split(key, 18)

    def nrm(k, shape, scale):
        return jax.random.normal(k, shape, jnp.float32) * scale

    def gain(k, shape):
        return 1.0 + 0.01 * jax.random.normal(k, shape, jnp.float32)

    return {
        'x': jax.random.normal(ks[0], (BATCH, SEQ, D_MODEL), jnp.float32),
        'norm_g': gain(ks[1], (DEPTH, D_MODEL)),
        'w_in': nrm(ks[2], (DEPTH, D_MODEL, IN_COLS), D_MODEL ** -0.5),
        'a_lam_q1': nrm(ks[3], (DEPTH, HEAD_DIM), 0.1),
        'a_lam_k1': nrm(ks[4], (DEPTH, HEAD_DIM), 0.1),
        'a_lam_q2': nrm(ks[5], (DEPTH, HEAD_DIM), 0.1),
        'a_lam_k2': nrm(ks[6], (DEPTH, HEAD_DIM), 0.1),
        'a_subln_g': gain(ks[7], (DEPTH, 2 * HEAD_DIM)),
        'b_q_norm_g': gain(ks[8], (DEPTH, B_Q_LORA)),
        'b_kv_norm_g': gain(ks[9], (DEPTH, B_KV_LORA)),
        'b_w_uq': nrm(ks[10], (DEPTH, B_Q_LORA, B_HEADS * (B_NOPE + B_ROPE)), B_Q_LORA ** -0.5),
        'b_w_ukv': nrm(ks[11], (DEPTH, B_KV_LORA, B_HEADS * (B_NOPE + B_V)), B_KV_LORA ** -0.5),
        'c_rpb': nrm(ks[12], (DEPTH, C_HEADS, 2 * NA_ROWS_MAX - 1, 2 * NA_COLS - 1), 0.1),
        'w_br_a': nrm(ks[13], (DEPTH, A_WIDTH, D_MODEL), A_WIDTH ** -0.5),
        'w_br_b': nrm(ks[14], (DEPTH, B_WIDTH, D_MODEL), B_WIDTH ** -0.5),
        'w_br_c': nrm(ks[15], (DEPTH, C_WIDTH, D_MODEL), C_WIDTH ** -0.5),
        'w_o': nrm(ks[16], (DEPTH, D_MODEL, D_MODEL), D_MODEL ** -0.5),
        'final_norm_g': gain(ks[17], (D_MODEL,)),
    }


def reference(x, norm_g, w_in, a_lam_q1, a_lam_k1, a_lam_q2, a_lam_k2, a_subln_g,
              b_q_norm_g, b_kv_norm_g, b_w_uq, b_w_ukv, c_rpb, w_br_a, w_br_b, w_br_c,
              w_o, final_norm_g):
    b, s_len, _ = x.shape
    for l in range(DEPTH):
        h = rmsnorm(x, norm_g[l])
        z = h @ w_in[l]
        (qa, ka, va, ga, cq, ckv, krope, gb, qc, kc, vc, gc, sa, sb, sc) = jnp.split(z, SPLIT_POINTS, axis=-1)

        lam_init = 0.8 - 0.6 * math.exp(-0.3 * l)
        lam = (jnp.exp(jnp.sum(a_lam_q1[l].astype(jnp.float32) * a_lam_k1[l].astype(jnp.float32)))
               - jnp.exp(jnp.sum(a_lam_q2[l].astype(jnp.float32) * a_lam_k2[l].astype(jnp.float32)))
               + lam_init)
        ya = diff_attention(qa.reshape(b, s_len, A_HEADS, 2, HEAD_DIM),
                            ka.reshape(b, s_len, A_HEADS, 2, HEAD_DIM),
                            va.reshape(b, s_len, A_HEADS, 2 * HEAD_DIM),
                            lam, lam_init, a_subln_g[l]) * jax.nn.silu(ga)

        yb = mla(cq, ckv, krope, b_q_norm_g[l], b_kv_norm_g[l], b_w_uq[l], b_w_ukv[l]) * jax.nn.silu(gb)

        yc = neighbourhood_attention(qc.reshape(b, s_len, C_HEADS, HEAD_DIM),
                                     kc.reshape(b, s_len, C_HEADS, HEAD_DIM),
                                     vc.reshape(b, s_len, C_HEADS, HEAD_DIM),
                                     c_rpb[l]) * jax.nn.silu(gc)

        merged = (jax.nn.sigmoid(sa) * (ya @ w_br_a[l])
                  + jax.nn.sigmoid(sb) * (yb @ w_br_b[l])
                  + jax.nn.sigmoid(sc) * (yc @ w_br_c[l]))
        x = x + merged @ w_o[l]
    return rmsnorm(x, final_norm_g)
```

```python
import math
import numpy as np
import ml_dtypes
import concourse.bass as bass
import concourse.mybir as mybir
from concourse.bass_utils import run_bass_kernel_spmd

F32 = mybir.dt.float32
BF16 = mybir.dt.bfloat16
AF = mybir.ActivationFunctionType
ALU = mybir.AluOpType
AX = mybir.AxisListType

NR = 8
DQ = ["sync"]
HD = 128
GRID_W = 64
EPS = 1e-6
NEG = -30000.0


class Cfg:
    def __init__(self, D=4096, T=2048, L=4, QL=1536, KVL=512):
        self.D, self.T, self.L, self.QL, self.KVL = D, T, L, QL, KVL
        self.S = NR * T
        self.KC = D // 128
        self.QC = QL // 128
        self.KVC = KVL // 128
        self.NB = T // 512
        c = 0
        self.off = {}
        for name, n in (("qa", 16), ("ka", 16), ("va", 16), ("ga", 16), ("cq", self.QC), ("ckv", self.KVC),
                        ("gb", 16), ("qc", 16), ("kc", 16), ("vc", 16), ("gc", 16),
                        ("sa", self.KC), ("sb", self.KC), ("sc", self.KC), ("kr", 1)):
            self.off[name] = (c, n)
            c += n
        self.NCH = c
        self.NCOLX = c * 128
        T_ = T
        o = 0
        self.pay = {}
        for name, n in (("qa", 256 * T_), ("ka", 256 * T_), ("va", 256 * T_), ("qbn", 256 * T_), ("qbr", 128 * T_),
                        ("kbn", 256 * T_), ("vb", 256 * T_), ("kr", 64 * T_), ("qc", 256 * T_), ("kc", 256 * T_),
                        ("vc", 256 * T_)):
            self.pay[name] = o
            o += n
        self.PAY = o
        self.PAYB = 768 * T_
        self.NQB = self.S // 512
        self.NKT = self.S // 128


ENGS = ("tensor", "vector", "scalar", "gpsimd", "sync")
DMA_ENGS = ("gpsimd", "sync")


class Sems:
    def __init__(self, nc, stack, n_dma=40):
        cc = [stack.enter_context(nc.semaphore("sd_cc_%d" % i)) for i in range(8)]
        self.eng = {e: stack.enter_context(nc.semaphore("se_" + e)) for e in ENGS}
        self.eng_cnt = {e: 0 for e in ENGS}
        npool = {"gpsimd": 18, "sync": 64}
        self.dma = {q: [stack.enter_context(nc.semaphore("sd_%s_%d" % (q, i))) for i in range(npool[q])] for q in DMA_ENGS}
        self.dma["cc"] = cc
        self.dma_cnt = {q: [0] * 64 for q in DMA_ENGS + ("cc",)}


class Phase:
    def __init__(self, nc, sems, name):
        self.nc, self.sems, self.name = nc, sems, name
        self.ops = []
        self.last_w = {}
        self.readers = {}
        self.keymap = {}

    def op(self, eng, fn, reads=(), writes=(), key=None, inc=None):
        i = len(self.ops)
        deps = set()
        for b in reads:
            if b in self.last_w:
                deps.add(self.last_w[b])
        for b in writes:
            if b in self.last_w:
                deps.add(self.last_w[b])
            for r in self.readers.get(b, ()):
                deps.add(r)
        deps.discard(i)
        for b in writes:
            self.last_w[b] = i
            self.readers[b] = []
        for b in reads:
            self.readers.setdefault(b, []).append(i)
        self.ops.append(dict(eng=eng, fn=fn, deps=deps, key=key, inc=(inc if inc is not None else 16),
                             signal=False, sem=None, val=None))
        return i

    def dma(self, eng, out_fn, in_fn, reads, writes, key):
        return self.op(eng, lambda e: e.dma_start(out=out_fn(), in_=in_fn()), reads, writes, key=key)

    def emit(self):
        nc, sems, ops = self.nc, self.sems, self.ops
        for i, o in enumerate(ops):
            keep = set()
            for d in o["deps"]:
                p = ops[d]
                if p["key"] is None and p["eng"] == o["eng"] and o["key"] is None and o["eng"] == "tensor":
                    continue
                keep.add(d)
            o["deps"] = keep
            for d in keep:
                ops[d]["signal"] = True
        for o in ops:
            if o["key"] is not None:
                o["signal"] = True
        for o in ops:
            if not o["signal"]:
                continue
            if o["key"] is None:
                sems.eng_cnt[o["eng"]] += 1
                o["sem"], o["val"] = sems.eng[o["eng"]], sems.eng_cnt[o["eng"]]
            else:
                q = o["eng"] if o["inc"] == 16 else "cc"
                km = self.keymap.setdefault(q, {})
                if o["key"] not in km:
                    assert len(km) < len(sems.dma[q]), "out of dma semaphores in phase " + self.name
                    km[o["key"]] = len(km)
                k = km[o["key"]]
                sems.dma_cnt[q][k] += o["inc"]
                o["sem"], o["val"] = sems.dma[q][k], sems.dma_cnt[q][k]
        waited_async = set()
        per_eng = {e: [] for e in ENGS}
        for i, o in enumerate(ops):
            per_eng[o["eng"]].append(i)
        for o in ops:
            for d in o["deps"]:
                if ops[d]["key"] is not None:
                    waited_async.add(d)
        tail = {e: [] for e in ENGS}
        for i, o in enumerate(ops):
            if o["key"] is not None and i not in waited_async:
                tail[o["eng"]].append(i)
        self.pid_cache = {}
        self.cur = None

        def run_engine(ename):
            def body(e):
                self.pid_cache[ename] = None
                seen = {}
                for i in per_eng[ename]:
                    o = ops[i]
                    need = {}
                    for d in o["deps"]:
                        p = ops[d]
                        sid = id(p["sem"])
                        if sid not in need or need[sid][1] < p["val"]:
                            need[sid] = (p["sem"], p["val"])
                    for sid, (sm_, v_) in need.items():
                        if seen.get(sid, -1) >= v_:
                            continue
                        e.wait_ge(sm_, v_)
                        seen[sid] = v_
                    self.cur = ename
                    ins = o["fn"](e)
                    if o["signal"]:
                        ins.then_inc(o["sem"], o["inc"] if o["key"] is not None else 1)
                need = {}
                for i in tail[ename]:
                    p = ops[i]
                    sid = id(p["sem"])
                    if sid not in need or need[sid][1] < p["val"]:
                        need[sid] = (p["sem"], p["val"])
                for sid, (sm_, v_) in need.items():
                    if seen.get(sid, -1) >= v_:
                        continue
                    e.wait_ge(sm_, v_)
                    seen[sid] = v_
            return body

        with nc.Block() as block:
            for ename in ENGS:
                if per_eng[ename]:
                    getattr(block, ename)(run_engine(ename))
        self.ops = []

    def pid(self, e, ename):
        if self.pid_cache.get(ename) is None:
            self.pid_cache[ename] = e.partition_id()
        return self.pid_cache[ename]


class RR:
    def __init__(self, items):
        self.items, self.i = list(items), 0

    def next(self):
        v = self.items[self.i % len(self.items)]
        self.i += 1
        return v


def lam_init_of(l):
    return 0.8 - 0.6 * math.exp(-0.3 * l)


def build_program(cfg):
    from contextlib import ExitStack
    nc = bass.Bass("TRN2", target_bir_lowering=False)
    D, T, L, KC, QC, KVC, NB, S = cfg.D, cfg.T, cfg.L, cfg.KC, cfg.QC, cfg.KVC, cfg.NB, cfg.S
    NQB, NKT, PAY, PAYB, NCOLX = cfg.NQB, cfg.NKT, cfg.PAY, cfg.PAYB, cfg.NCOLX
    QL, KVL = cfg.QL, cfg.KVL
    TH = min(T, 1024)
    KG = min(4, KC)
    NH = T // TH
    groups = [list(range(NR))]

    _uid = [0]

    def sbt(name, shape, dt):
        _uid[0] += 1
        return nc.sbuf_tensor("%s_%d" % (name, _uid[0]), shape, dt)

    def pst(name, shape, dt):
        _uid[0] += 1
        return nc.psum_tensor("%s_%d" % (name, _uid[0]), shape, dt)

    def din(name, shape, dt=F32):
        return nc.dram_tensor(name, list(shape), dt, kind="ExternalInput").ap()

    def dint(name, shape, dt=BF16):
        if name in getattr(cfg, "debug_outs", ()):
            return nc.dram_tensor(name, list(shape), dt, kind="ExternalOutput").ap()
        return nc.dram_tensor(name, list(shape), dt).ap()

    x_in = din("x", [T, D])
    out_ap = nc.dram_tensor("out", [T, D], F32, kind="ExternalOutput").ap()
    wsrc = {
        "wuq": din("w_uq_sh", [QL // 8, L * 4096]),
        "wukv": din("w_ukv_sh", [KVL // 8, L * 4096]),
        "wba": din("w_br_a_sh", [256, L * D]),
        "wbb": din("w_br_b_sh", [256, L * D]),
        "wbc": din("w_br_c_sh", [256, L * D]),
        "wo": din("w_o_sh", [D // 8, L * D]),
    }
    CS = (cfg.NCH // 8) * 4
    cfg.CS = CS
    win_parts = [(0, CS), (CS, cfg.NCH)]
    for l_ in range(L):
        for pi, (a_, b_) in enumerate(win_parts):
            wsrc["win%d_%d" % (l_, pi)] = din("w_in_sh%d_%d" % (l_, pi), [D // 8, (b_ - a_) * 128])
    g_in = din("g_in", [128, L * KC])
    g_q = din("g_q", [128, L * QC])
    g_kv = din("g_kv", [128, L * KVC])
    g_sub = din("g_sub", [128, L * 256])
    lamv = din("lamv", [128, L * 512])
    g_fin = din("g_fin", [128, D])
    cs1_in = din("cs1", [64, T])
    cs2_in = din("cs2", [64, T])
    tb_in = din("tb_c", [L * 2, 128, 22 * 64])
    rl_in = din("rl_a", [128, 512])
    dg_in = din("dg_a", [128, 4 * 512])
    NCB = NKT + 8
    cb_in = din("cb_a", [128, NCB])
    ident_in = din("ident", [128, 128], BF16)
    ones_in = din("ones", [128, 128], BF16)
    rowsel_in = din("rowsel", [16, 8 * 128], BF16)
    rows = S // GRID_W
    emat_cls, emat_list = [], []
    for qb in range(NQB):
        r0 = 8 * qb
        E = np.full((16, 512), NEG, np.float32)
        for i in range(16):
            kr = r0 - 4 + i
            if kr < 0 or kr >= rows:
                continue
            for rq in range(8):
                r = r0 + rq
                rs = min(max(r - 4, 0), rows - 8)
                if rs <= kr <= rs + 7:
                    E[i, rq * 64:(rq + 1) * 64] = 0.0
        for ci, Ee in enumerate(emat_list):
            if np.array_equal(Ee, E):
                emat_cls.append(ci)
                break
        else:
            emat_cls.append(len(emat_list))
            emat_list.append(E)
    NCLS = len(emat_list)
    emat_in = din("emat", [16, NCLS * 512], BF16)
    cfg.emat_np = np.concatenate(emat_list, axis=1)

    wsh = {k: dint("wsh_" + k, v.shape) for k, v in wsrc.items()}
    wfull = {k: dint("wfull_" + k, [v.shape[0] * 8, v.shape[1]]) for k, v in wsrc.items()}
    XRES = dint("xres", [T, D], F32)
    HT = dint("ht", [KC, 128, T])
    MT = dint("mt", [KC, 128, T])
    GATES = dint("gates", [48, 128, T])
    SIG = dint("sig", [KC, 3, 128, T])
    CQT = dint("cqt", [QC, 128, T])
    CKVT = dint("ckvt", [KVC, 128, T])
    SEND = dint("send", [8, PAY])
    RECV = dint("recv_dummy", [1, 64])
    PARTS = [(0, cfg.pay["qbn"]), (cfg.pay["qbn"], cfg.pay["qc"]), (cfg.pay["qc"], PAY)]
    SENDP = [dint("sendp%d" % i, [8, b_ - a_]) for i, (a_, b_) in enumerate(PARTS)]
    RECVP = [dint("recvp%d" % i, [64, b_ - a_]) for i, (a_, b_) in enumerate(PARTS)]
    SENDB = dint("sendb", [8, PAYB])
    RECVB = dint("recvb", [64, PAYB])

    def wl(name, l, ncols):
        return wfull[name][:, l * ncols:(l + 1) * ncols]

    def send_v(d, name, sub, n, pat, **kw):
        o = cfg.pay[name] + sub
        for i, (a_, b_) in enumerate(PARTS):
            if a_ <= o < b_:
                return SENDP[i][d:d + 1, o - a_:o - a_ + n].rearrange("o " + pat, **kw)
        raise AssertionError

    stack = ExitStack()
    with stack:
        sems = Sems(nc, stack)

        MINE = dint("mine", [8, PAY])
        MINEB = dint("mineb", [8, PAYB])
        mine_of = {id(RECV): MINE.rearrange("(o s) n -> o s n", o=1), id(RECVB): MINEB.rearrange("(o s) n -> o s n", o=1)}

        def recv_dyn(ph, e, buf, s, off, n):
            pid = ph.pid(e, ph.cur)
            return buf.rearrange("(s d) n -> d s n", d=8)[bass.ds(pid, 1), s:s + 1, off:off + n]

        def recv_mine(ph, e, buf, s, off, n):
            return mine_of[id(buf)][0:1, s:s + 1, off:off + n]

        def weight_ops(ph):
            for k in ([] if getattr(cfg, "skip_w", False) else wsrc):
                rws, cols = wsrc[k].shape
                step = max(1, (1 << 22) // cols)
                r = 0
                pieces = []
                while r < rws:
                    r2 = min(rws, r + step)
                    ph.op("gpsimd", (lambda e, k=k, r=r, r2=r2: e.dma_start(out=wsh[k][r:r2, :], in_=wsrc[k][r:r2, :])),
                          reads=[], writes=[("wsh", k, r)], key=("wc", k))
                    pieces.append(("wsh", k, r))
                    r = r2
                ph.op("gpsimd", (lambda e, k=k: e.collective_compute("AllGather", ALU.bypass, replica_groups=groups,
                                                                     ins=[wsh[k]], outs=[wfull[k]])),
                      reads=pieces, writes=[("wfull", k), ("agchain",)], key=("ag",), inc=1)

        def gather_ops(ph, parts, mine, tag):
            dq = RR(DQ)
            for pi, (src, dst, off, n) in enumerate(parts):
                ph.op("gpsimd", (lambda e, src=src, dst=dst: e.collective_compute("AllGather", ALU.bypass, replica_groups=groups,
                                                                                  ins=[src], outs=[dst])),
                      writes=[("g", tag, pi), ("agchain",)], key=("ag",), inc=1)
                RW = 16384
                assert n % RW == 0
                ph.op("sync", (lambda e, dst=dst, off=off, n=n: e.dma_start(
                    out=mine[:, off:off + n].rearrange("s (p f) -> s p f", f=RW),
                    in_=dst.rearrange("(s d) n -> d s n", d=8)[bass.ds(ph.pid(e, ph.cur), 1), :, 0:n].rearrange(
                        "o s (p f) -> (o s) p f", f=RW))),
                    reads=[("g", tag, pi)], writes=[("mine", s, pi) for s in range(NR)], key=("mine", pi))

        def load_consts(ph, st, names, cols=None):
            res = {}
            cols = cols or {}
            spec = {"ident": (ident_in, [128, 128], BF16), "ones": (ones_in, [128, 128], BF16),
                    "rowsel": (rowsel_in, [16, 8 * 128], BF16), "emat": (emat_in, [16, NCLS * 512], BF16),
                    "g_in": (g_in, [128, L * KC], F32), "g_q": (g_q, [128, L * QC], F32),
                    "g_kv": (g_kv, [128, L * KVC], F32), "g_sub": (g_sub, [128, L * 256], F32),
                    "lamv": (lamv, [128, L * 512], F32), "g_fin": (g_fin, [128, D], F32),
                    "cs1": (cs1_in, [64, T], F32), "cs2": (cs2_in, [64, T], F32),
                    "rl": (rl_in, [128, 512], F32), "dg": (dg_in, [128, 4 * 512], F32), "cb": (cb_in, [128, NCB], F32)}
            for nm in names:
                src, shp, dt = spec[nm]
                if nm in cols:
                    lo, hi = cols[nm]
                    src, shp = src[:, lo:hi], [shp[0], hi - lo]
                t = st.enter_context(sbt("c_" + nm + "_" + ph.name, shp, dt))
                ph.op("sync", (lambda e, t=t, src=src: e.dma_start(out=t[:], in_=src)), writes=[("c", nm)], key=("c", nm))
                res[nm] = t
            return res

        def rstd_ops(ph, dst, src, n, rd, wr):
            ph.op("scalar", lambda e: e.activation(out=dst, in_=src, func=AF.Ln, bias=EPS, scale=1.0 / n), reads=rd, writes=wr)
            ph.op("scalar", lambda e: e.activation(out=dst, in_=dst, func=AF.Exp, scale=-0.5), reads=wr, writes=wr)

        def phase_norm(l, xsrc):
            ph = Phase(nc, sems, "n%d" % l)
            with ExitStack() as st:
                C = load_consts(ph, st, ["ident", "g_in"])
                xt = [st.enter_context(sbt("xt%d" % i, [128, D], F32)) for i in range(2)]
                xn = [st.enter_context(sbt("xn%d" % i, [128, D], BF16)) for i in range(2)]
                hs = [st.enter_context(sbt("hs%d" % i, [128, KC, 512], BF16)) for i in range(2)]
                ss = st.enter_context(sbt("ss", [128, T // 128], F32))
                pt = [st.enter_context(pst("pt%d" % i, [128, 512], BF16)) for i in range(4)]
                ph.op("vector", lambda e: e.memset(ss[:], 0.0), writes=[("ss", tt) for tt in range(T // 128)])
                for tt in range(T // 128):
                    sl, hsl = tt % 2, (tt // 4) % 2
                    ph.op("sync", (lambda e, sl=sl, tt=tt: e.dma_start(out=xt[sl][:], in_=xsrc[tt * 128:(tt + 1) * 128, :])),
                          writes=[("xt", sl)], key=("xt", sl))
                    ph.op("scalar", (lambda e, sl=sl, tt=tt: e.activation(out=xn[sl][:], in_=xt[sl][:], func=AF.Square,
                                                                          accum_out=ss[:, tt:tt + 1])),
                          reads=[("xt", sl)], writes=[("xn", sl), ("ss", tt)])
                    rstd_ops(ph, ss[:, tt:tt + 1], ss[:, tt:tt + 1], D, [("ss", tt)], [("ss", tt)])
                    ph.op("vector", (lambda e, sl=sl, tt=tt: e.tensor_scalar(out=xn[sl][:], in0=xt[sl][:], scalar1=ss[:, tt:tt + 1],
                                                                             scalar2=None, op0=ALU.mult)),
                          reads=[("xt", sl), ("ss", tt)], writes=[("xn", sl)])
                    for kc in range(KC):
                        ps = kc % 4
                        ph.op("tensor", (lambda e, sl=sl, kc=kc, ps=ps: e.transpose(out=pt[ps][:, 0:128], in_=xn[sl][:, kc * 128:(kc + 1) * 128],
                                                                                    identity=C["ident"][:])),
                              reads=[("xn", sl), ("c", "ident")], writes=[("pt", ps)])
                        eng = "vector" if kc % 2 == 0 else "scalar"
                        dst = lambda hsl=hsl, kc=kc, tt=tt: hs[hsl][:, kc, (tt % 4) * 128:(tt % 4 + 1) * 128]
                        gsc = lambda kc=kc: C["g_in"][:, l * KC + kc:l * KC + kc + 1]
                        if eng == "vector":
                            ph.op("vector", (lambda e, ps=ps, dst=dst, gsc=gsc: e.tensor_scalar(out=dst(), in0=pt[ps][:, 0:128], scalar1=gsc(),
                                                                                                scalar2=None, op0=ALU.mult)),
                                  reads=[("pt", ps), ("c", "g_in")], writes=[("hs", hsl, kc, tt % 4)])
                        else:
                            ph.op("scalar", (lambda e, ps=ps, dst=dst, gsc=gsc: e.activation(out=dst(), in_=pt[ps][:, 0:128], func=AF.Identity,
                                                                                             scale=gsc())),
                                  reads=[("pt", ps), ("c", "g_in")], writes=[("hs", hsl, kc, tt % 4)])
                    if tt % 4 == 3 or tt == T // 128 - 1:
                        t0 = (tt // 4) * 512
                        nt = (tt % 4 + 1) * 128
                        ph.op(DQ[-1], (lambda e, hsl=hsl, t0=t0, nt=nt: e.dma_start(
                            out=HT[:, :, t0:t0 + nt].rearrange("k p t -> p k t"), in_=hs[hsl][:, :, 0:nt])),
                            reads=[("hs", hsl, kc, q) for kc in range(KC) for q in range(4)], writes=[],
                            key=("hs", hsl))
                        for kc in range(KC):
                            for q in range(4):
                                ph.readers.setdefault(("hs", hsl, kc, q), [])
                ph.emit()

        def chunk_kind(c):
            for nm, (o, n) in cfg.off.items():
                if o <= c < o + n:
                    return nm, c - o
            raise AssertionError

        def phase_inproj(l):
            ph = Phase(nc, sems, "p%d" % l)
            Wp = [wfull["win%d_%d" % (l, pi)] for pi in range(2)]
            with ExitStack() as st:
                C = load_consts(ph, st, [] if "nocs" in getattr(cfg, "dbg_flags", ()) else ["cs1", "cs2"])
                hT = st.enter_context(sbt("hT", [128, KC, TH], BF16))
                wt = [st.enter_context(sbt("wt%d" % i, [128, KC, 512], BF16)) for i in range(2)]
                stg = [st.enter_context(sbt("stg%d" % i, [128, 512], BF16)) for i in range(4)]
                t1 = st.enter_context(sbt("t1", [64, 512], F32))
                t2 = st.enter_context(sbt("t2", [64, 512], F32))
                ps = [st.enter_context(pst("ps%d" % i, [128, 512], F32)) for i in range(6)]
                dq = RR(DQ)
                psr, sgr = RR(range(6)), RR(range(4))
                evr = RR(["vector", "scalar"])
                NG = (cfg.NCH + 3) // 4
                gi = 0
                for hh in range(NH):
                    tok0 = hh * TH
                    for k4 in range(0, KC, KG):
                        if "noht" in getattr(cfg, "dbg_flags", ()):
                            break
                        ph.op(dq.next(), (lambda e, k4=k4, tok0=tok0: e.dma_start(
                            out=hT[:, k4:k4 + KG, :], in_=HT[k4:k4 + KG, :, tok0:tok0 + TH].rearrange("k p t -> p k t"))),
                            writes=[("hT", k4 // KG)], key=("hT", k4 // KG))
                    for g in range(NG):
                        if "nowt" in getattr(cfg, "dbg_flags", ()):
                            break
                        c0 = g * 4
                        ncg = min(4, cfg.NCH - c0)
                        ws = gi % 2
                        gi += 1
                        ph.op(dq.next(), (lambda e, ws=ws, c0=c0, ncg=ncg: e.dma_start(
                            out=wt[ws][:, :, 0:ncg * 128],
                            in_=(Wp[0][:, c0 * 128:(c0 + ncg) * 128] if c0 < cfg.CS else
                                 Wp[1][:, (c0 - cfg.CS) * 128:(c0 - cfg.CS + ncg) * 128]).rearrange("(k p) c -> p k c", p=128))),
                            writes=[("wt", ws)], key=("wt", ws))
                        kind0, _ = chunk_kind(c0)
                        allow = getattr(cfg, "dbg_kinds", None)
                        if kind0 in ("va", "vc") and allow is not None and kind0 not in allow:
                            continue
                        if kind0 in ("va", "vc"):
                            assert ncg == 4 and chunk_kind(c0 + 3)[0] == kind0
                            _, ci = chunk_kind(c0)
                            for tt in range(TH // 128):
                                p_ = psr.next()
                                for kc in range(KC):
                                    ph.op("tensor", (lambda e, p_=p_, kc=kc, tt=tt, ws=ws: e.matmul(
                                        ps[p_][:], hT[:, kc, tt * 128:(tt + 1) * 128], wt[ws][:, kc, :], start=(kc == 0), stop=(kc == KC - 1))),
                                        reads=[("hT", kc // KG), ("wt", ws)], writes=[("ps", p_)])
                                s_ = sgr.next()
                                ev = evr.next()
                                if ev == "vector":
                                    ph.op("vector", (lambda e, p_=p_, s_=s_: e.tensor_copy(out=stg[s_][:], in_=ps[p_][:])),
                                          reads=[("ps", p_)], writes=[("stg", s_)])
                                else:
                                    ph.op("scalar", (lambda e, p_=p_, s_=s_: e.activation(out=stg[s_][:], in_=ps[p_][:], func=AF.Copy)),
                                          reads=[("ps", p_)], writes=[("stg", s_)])
                                for half in range(2):
                                    dest = ci // 4 * 2 + half
                                    dst = lambda dest=dest, tt=tt, tok0=tok0, kind0=kind0: send_v(
                                        dest, kind0, 0, 256 * T, "(t e) -> t (o e)", e=256)[tok0 + tt * 128:tok0 + (tt + 1) * 128, :]
                                    ph.op(dq.next(), (lambda e, dst=dst, s_=s_, half=half: e.dma_start(
                                        out=dst(), in_=stg[s_][:, half * 256:(half + 1) * 256])),
                                        reads=[("stg", s_)], writes=[], key=("stg", s_))
                            continue
                        for cc in range(ncg):
                            c = c0 + cc
                            kind, ci = chunk_kind(c)
                            if allow is not None and kind not in allow:
                                continue
                            for tb in range(TH // 512):
                                tcol = tok0 + tb * 512
                                if kind == "kr":
                                    pa, pb = psr.next(), psr.next()
                                    for (p_, lo) in ((pa, 0), (pb, 64)):
                                        for kc in range(KC):
                                            ph.op("tensor", (lambda e, p_=p_, lo=lo, kc=kc, ws=ws, tb=tb: e.matmul(
                                                ps[p_][0:64, :], wt[ws][:, kc, cc * 128 + lo:cc * 128 + lo + 64],
                                                hT[:, kc, tb * 512:(tb + 1) * 512], start=(kc == 0), stop=(kc == KC - 1))),
                                                reads=[("hT", kc // KG), ("wt", ws)], writes=[("ps", p_)])
                                    s_ = sgr.next()
                                    ph.op("vector", (lambda e, pa=pa, tcol=tcol: e.tensor_tensor(
                                        out=t1[:], in0=ps[pa][0:64, :], in1=C["cs1"][:, tcol:tcol + 512], op=ALU.mult)),
                                        reads=[("ps", pa), ("c", "cs1")], writes=[("t1",)])
                                    ph.op("vector", (lambda e, pb=pb, tcol=tcol: e.tensor_tensor(
                                        out=t2[:], in0=ps[pb][0:64, :], in1=C["cs2"][:, tcol:tcol + 512], op=ALU.mult)),
                                        reads=[("ps", pb), ("c", "cs2")], writes=[("t2",)])
                                    ph.op("vector", (lambda e, s_=s_: e.tensor_tensor(out=stg[s_][0:64, :], in0=t1[:], in1=t2[:], op=ALU.add)),
                                          reads=[("t1",), ("t2",)], writes=[("stg", s_)])
                                    for d in range(NR):
                                        dst = lambda d=d, tcol=tcol: send_v(d, "kr", 0, 64 * T, "(p t) -> p (o t)", p=64)[:, tcol:tcol + 512]
                                        ph.op(dq.next(), (lambda e, dst=dst, s_=s_: e.dma_start(out=dst(), in_=stg[s_][0:64, :])),
                                              reads=[("stg", s_)], writes=[], key=("stg", s_))
                                    continue
                                p_ = psr.next()
                                for kc in range(KC):
                                    ph.op("tensor", (lambda e, p_=p_, kc=kc, ws=ws, cc=cc, tb=tb: e.matmul(
                                        ps[p_][:], wt[ws][:, kc, cc * 128:(cc + 1) * 128], hT[:, kc, tb * 512:(tb + 1) * 512],
                                        start=(kc == 0), stop=(kc == KC - 1))),
                                        reads=[("hT", kc // KG), ("wt", ws)], writes=[("ps", p_)])
                                s_ = sgr.next()
                                if kind in ("ga", "gb", "gc"):
                                    func, ev = AF.Silu, "scalar"
                                elif kind in ("sa", "sb", "sc"):
                                    func, ev = AF.Sigmoid, "scalar"
                                else:
                                    func, ev = AF.Copy, evr.next()
                                if ev == "vector":
                                    ph.op("vector", (lambda e, p_=p_, s_=s_: e.tensor_copy(out=stg[s_][:], in_=ps[p_][:])),
                                          reads=[("ps", p_)], writes=[("stg", s_)])
                                else:
                                    ph.op("scalar", (lambda e, p_=p_, s_=s_, func=func: e.activation(out=stg[s_][:], in_=ps[p_][:], func=func)),
                                          reads=[("ps", p_)], writes=[("stg", s_)])
                                if kind in ("qa", "ka"):
                                    h, m = ci // 2, ci % 2
                                    dst = lambda h=h, m=m, kind=kind, tcol=tcol: send_v(
                                        h, kind, m * 128 * T, 128 * T, "(p t) -> p (o t)", p=128)[:, tcol:tcol + 512]
                                elif kind in ("qc", "kc"):
                                    dd, sl = ci // 2, ci % 2
                                    dst = lambda dd=dd, sl=sl, kind=kind, tcol=tcol: send_v(
                                        dd, kind, sl * 128 * T, 128 * T, "(p t) -> p (o t)", p=128)[:, tcol:tcol + 512]
                                elif kind in ("ga", "gb", "gc"):
                                    br = ("ga", "gb", "gc").index(kind)
                                    gidx = (ci // 2) * 6 + br * 2 + ci % 2
                                    dst = lambda gidx=gidx, tcol=tcol: GATES[gidx, :, tcol:tcol + 512]
                                elif kind in ("sa", "sb", "sc"):
                                    br = ("sa", "sb", "sc").index(kind)
                                    dst = lambda br=br, ci=ci, tcol=tcol: SIG[ci, br, :, tcol:tcol + 512]
                                elif kind == "cq":
                                    dst = lambda ci=ci, tcol=tcol: CQT[ci, :, tcol:tcol + 512]
                                elif kind == "ckv":
                                    dst = lambda ci=ci, tcol=tcol: CKVT[ci, :, tcol:tcol + 512]
                                else:
                                    raise AssertionError(kind)
                                ph.op(dq.next(), (lambda e, dst=dst, s_=s_: e.dma_start(out=dst(), in_=stg[s_][:])),
                                      reads=[("stg", s_)], writes=[], key=("stg", s_))
                ph.emit()

        def phase_bprep(l):
            ph = Phase(nc, sems, "b%d" % l)
            WQ = wl("wuq", l, 4096)
            WKV = wl("wukv", l, 4096)
            with ExitStack() as st:
                C = load_consts(ph, st, ["ones", "g_q", "g_kv", "cs1", "cs2"])
                cq = st.enter_context(sbt("cq", [128, QC, 512], BF16))
                cqs = st.enter_context(sbt("cqs", [128, QC, 512], BF16))
                cqn = st.enter_context(sbt("cqn", [128, QC, 512], BF16))
                ckv = st.enter_context(sbt("ckv", [128, KVC, 512], BF16))
                ckvs = st.enter_context(sbt("ckvs", [128, KVC, 512], BF16))
                ckvn = st.enter_context(sbt("ckvn", [128, KVC, 512], BF16))
                rq = st.enter_context(sbt("rq", [128, 512], F32))
                rk = st.enter_context(sbt("rk", [128, 512], F32))
                wq = [st.enter_context(sbt("wq%d" % i, [128, QC, 256], BF16)) for i in range(2)]
                wkv = st.enter_context(sbt("wkv", [128, KVC, 4096], BF16))
                stg = [st.enter_context(sbt("stg%d" % i, [128, 512], BF16)) for i in range(4)]
                t1 = st.enter_context(sbt("t1", [64, 512], F32))
                t2 = st.enter_context(sbt("t2", [64, 512], F32))
                ps = [st.enter_context(pst("ps%d" % i, [128, 512], F32)) for i in range(6)]
                dq = RR(DQ)
                psr, sgr, evr = RR(range(6)), RR(range(4)), RR(["vector", "scalar"])
                ph.op("sync", lambda e: e.dma_start(out=wkv[:], in_=WKV.rearrange("(k p) c -> p k c", p=128)),
                      writes=[("wkv",)], key=("wkv",))

                def evac_store(p_, dst, rows=128):
                    s_ = sgr.next()
                    ev = evr.next()
                    if ev == "vector":
                        ph.op("vector", (lambda e: e.tensor_copy(out=stg[s_][0:rows, :], in_=ps[p_][0:rows, :])),
                              reads=[("ps", p_)], writes=[("stg", s_)])
                    else:
                        ph.op("scalar", (lambda e: e.activation(out=stg[s_][0:rows, :], in_=ps[p_][0:rows, :], func=AF.Copy)),
                              reads=[("ps", p_)], writes=[("stg", s_)])
                    ph.op(dq.next(), (lambda e: e.dma_start(out=dst(), in_=stg[s_][0:rows, :])),
                          reads=[("stg", s_)], writes=[], key=("stg", s_))

                def norm_latent(src_dram, nch, xin, xsq, xn, rbuf, gname, nfeat, tag, tcol):
                    ph.op(dq.next(), (lambda e: e.dma_start(out=xin[:], in_=src_dram[:, :, tcol:tcol + 512].rearrange("k p t -> p k t"))),
                          writes=[(tag, "in")], key=(tag, "in"))
                    ph.op("scalar", (lambda e: e.activation(out=xsq[:], in_=xin[:], func=AF.Square)),
                          reads=[(tag, "in")], writes=[(tag, "sq")])
                    p_ = psr.next()
                    for c in range(nch):
                        ph.op("tensor", (lambda e, c=c: e.matmul(ps[p_][:], C["ones"][:], xsq[:, c, :], start=(c == 0), stop=(c == nch - 1))),
                              reads=[(tag, "sq"), ("c", "ones")], writes=[("ps", p_)])
                    rstd_ops(ph, rbuf[:], ps[p_][:], nfeat, [("ps", p_)], [(tag, "r")])
                    for c in range(nch):
                        ph.op("vector", (lambda e, c=c: e.scalar_tensor_tensor(
                            out=xn[:, c, :], in0=xin[:, c, :], scalar=C[gname][:, l * nch + c:l * nch + c + 1], in1=rbuf[:],
                            op0=ALU.mult, op1=ALU.mult)),
                            reads=[(tag, "in"), (tag, "r"), ("c", gname)], writes=[(tag, "n")])

                for tb in range(NB):
                    tcol = tb * 512
                    norm_latent(CQT, QC, cq, cqs, cqn, rq, "g_q", QL, "cq", tcol)
                    norm_latent(CKVT, KVC, ckv, ckvs, ckvn, rk, "g_kv", KVL, "ckv", tcol)
                    for h in range(16):
                        ws = h % 2
                        ph.op(dq.next(), (lambda e, h=h, ws=ws: e.dma_start(
                            out=wq[ws][:], in_=WQ[:, h * 256:(h + 1) * 256].rearrange("(k p) c -> p k c", p=128))),
                            writes=[("wq", ws)], key=("wq", ws))
                        dd, sl = h // 2, h % 2
                        pn, pr, pw = psr.next(), psr.next(), psr.next()
                        for (p_, lo, m) in ((pn, 0, 128), (pr, 128, 64), (pw, 192, 64)):
                            for c in range(QC):
                                ph.op("tensor", (lambda e, p_=p_, lo=lo, m=m, c=c, ws=ws: e.matmul(
                                    ps[p_][0:m, :], wq[ws][:, c, lo:lo + m], cqn[:, c, :], start=(c == 0), stop=(c == QC - 1))),
                                    reads=[("wq", ws), ("cq", "n")], writes=[("ps", p_)])
                        evac_store(pn, lambda dd=dd, sl=sl, tcol=tcol: send_v(dd, "qbn", sl * 128 * T, 128 * T, "(p t) -> p (o t)", p=128)[:, tcol:tcol + 512])
                        s_ = sgr.next()
                        ph.op("vector", (lambda e, pr=pr, tcol=tcol: e.tensor_tensor(out=t1[:], in0=ps[pr][0:64, :], in1=C["cs1"][:, tcol:tcol + 512], op=ALU.mult)),
                              reads=[("ps", pr), ("c", "cs1")], writes=[("t1",)])
                        ph.op("vector", (lambda e, pw=pw, tcol=tcol: e.tensor_tensor(out=t2[:], in0=ps[pw][0:64, :], in1=C["cs2"][:, tcol:tcol + 512], op=ALU.mult)),
                              reads=[("ps", pw), ("c", "cs2")], writes=[("t2",)])
                        ph.op("vector", (lambda e, s_=s_: e.tensor_tensor(out=stg[s_][0:64, :], in0=t1[:], in1=t2[:], op=ALU.add)),
                              reads=[("t1",), ("t2",)], writes=[("stg", s_)])
                        ph.op(dq.next(), (lambda e, s_=s_, dd=dd, sl=sl, tcol=tcol: e.dma_start(
                            out=send_v(dd, "qbr", sl * 64 * T, 64 * T, "(p t) -> p (o t)", p=64)[:, tcol:tcol + 512], in_=stg[s_][0:64, :])),
                            reads=[("stg", s_)], writes=[], key=("stg", s_))
                        p_ = psr.next()
                        for c in range(KVC):
                            ph.op("tensor", (lambda e, p_=p_, c=c, h=h: e.matmul(
                                ps[p_][:], wkv[:, c, h * 128:(h + 1) * 128], ckvn[:, c, :], start=(c == 0), stop=(c == KVC - 1))),
                                reads=[("wkv",), ("ckv", "n")], writes=[("ps", p_)])
                        evac_store(p_, lambda dd=dd, sl=sl, tcol=tcol: send_v(dd, "kbn", sl * 128 * T, 128 * T, "(p t) -> p (o t)", p=128)[:, tcol:tcol + 512])
                    for tt in range(4):
                        for gq in range(4):
                            p_ = psr.next()
                            for c in range(KVC):
                                ph.op("tensor", (lambda e, p_=p_, c=c, tt=tt, gq=gq: e.matmul(
                                    ps[p_][:], ckvn[:, c, tt * 128:(tt + 1) * 128], wkv[:, c, 2048 + gq * 512:2048 + (gq + 1) * 512],
                                    start=(c == 0), stop=(c == KVC - 1))),
                                    reads=[("wkv",), ("ckv", "n")], writes=[("ps", p_)])
                            s_ = sgr.next()
                            ph.op("vector", (lambda e, p_=p_, s_=s_: e.tensor_copy(out=stg[s_][:], in_=ps[p_][:])),
                                  reads=[("ps", p_)], writes=[("stg", s_)])
                            for half in range(2):
                                dest = gq * 2 + half
                                ph.op(dq.next(), (lambda e, s_=s_, half=half, dest=dest, tt=tt, tcol=tcol: e.dma_start(
                                    out=send_v(dest, "vb", 0, 256 * T, "(t e) -> t (o e)", e=256)[tcol + tt * 128:tcol + (tt + 1) * 128, :],
                                    in_=stg[s_][:, half * 256:(half + 1) * 256])),
                                    reads=[("stg", s_)], writes=[], key=("stg", s_))
                ph.emit()

        def phase_attn_a(l):
            ph = Phase(nc, sems, "a%d" % l)
            scale = HD ** -0.5
            with ExitStack() as st:
                C = load_consts(ph, st, ["ident", "g_sub", "lamv", "rl", "dg", "cb"],
                                cols={"g_sub": (l * 256, (l + 1) * 256), "lamv": (l * 512, (l + 1) * 512)})
                kT = st.enter_context(sbt("kT", [128, 2, S], BF16))
                V = st.enter_context(sbt("V", [128, NKT, 257], BF16))
                qT = [st.enter_context(sbt("qT%d" % i, [128, 2, 512], BF16)) for i in range(2)]
                sb = [st.enter_context(sbt("sb%d" % i, [128, 512], F32)) for i in range(3)]
                pT = [st.enter_context(sbt("pT%d" % i, [128, 512], BF16)) for i in range(3)]
                om = st.enter_context(sbt("om", [128, 4, 2, 256], F32))
                junk = st.enter_context(sbt("junk", [128, 256], F32))
                obf = st.enter_context(sbt("obf", [128, 4, 256], BF16))
                oTs = [st.enter_context(sbt("oTs%d" % i, [128, 2, 512], BF16)) for i in range(2)]
                sm = st.enter_context(sbt("sm", [128, 16], F32))
                gsub = st.enter_context(sbt("gsub", [128, 256], F32))
                lt = st.enter_context(sbt("lt", [128, 8], F32))
                ljunk = st.enter_context(sbt("ljunk", [128, 128], F32))
                acc = [st.enter_context(pst("acc%d" % i, [128, 512], F32)) for i in range(4)]
                sc = [st.enter_context(pst("sc%d" % i, [128, 512], F32)) for i in range(3)]
                ptp = st.enter_context(pst("ptp", [128, 512], BF16))
                dq = RR(DQ)
                gather_ops(ph, [(SENDP[i], RECVP[i], a_, b_ - a_) for i, (a_, b_) in enumerate(PARTS)], MINE, "f")
                li = lam_init_of(l)
                ph.op("vector", lambda e: e.memset(lt[:], 0.0), writes=[("lt",)])
                for i in range(2):
                    ph.op("vector", (lambda e, i=i: e.tensor_tensor(out=ljunk[:], in0=C["lamv"][:, i * 256:i * 256 + 128],
                                                                    in1=C["lamv"][:, i * 256 + 128:i * 256 + 256], op=ALU.mult)),
                          reads=[("c", "lamv"), ("lt",)], writes=[("ljunk",)])
                    ph.op("vector", (lambda e, i=i: e.reduce_sum(out=lt[:, i:i + 1], in_=ljunk[:], axis=AX.X)),
                          reads=[("ljunk",)], writes=[("lt",)])
                ph.op("scalar", lambda e: e.activation(out=lt[:, 0:2], in_=lt[:, 0:2], func=AF.Exp), reads=[("lt",)], writes=[("lt",)])
                ph.op("vector", lambda e: e.tensor_tensor(out=lt[:, 2:3], in0=lt[:, 1:2], in1=lt[:, 0:1], op=ALU.subtract),
                      reads=[("lt",)], writes=[("lt",)])
                ph.op("vector", lambda e: e.tensor_scalar(out=lt[:, 2:3], in0=lt[:, 2:3], scalar1=-li, scalar2=None, op0=ALU.add),
                      reads=[("lt",)], writes=[("lt",)])
                ph.op("vector", lambda e: e.tensor_scalar(out=gsub[:], in0=C["g_sub"][:], scalar1=1.0 - li, scalar2=None,
                                                          op0=ALU.mult), reads=[("c", "g_sub")], writes=[("gsub",)])
                for s in range(NR):
                    ph.op(dq.next(), (lambda e, s=s: e.dma_start(
                        out=kT[:, :, s * T:(s + 1) * T],
                        in_=recv_mine(ph, e, RECV, s, cfg.pay["ka"], 256 * T).rearrange("o s (m p t) -> p (o s m) t", m=2, p=128))),
                        reads=[("mine", s, 0)], writes=[("kT", s)], key=("kT", s))
                    ph.op(dq.next(), (lambda e, s=s: e.dma_start(
                        out=V[:, s * (T // 128):(s + 1) * (T // 128), 0:256],
                        in_=recv_mine(ph, e, RECV, s, cfg.pay["va"], 256 * T).rearrange("o s (j p e) -> p (o s j) e", p=128, e=256))),
                        reads=[("mine", s, 0)], writes=[("V", s)], key=("V", s))
                ph.op("vector", lambda e: e.memset(V[:, :, 256:257], 1.0), writes=[("Vone",)])
                ph.op("vector", lambda e: e.memset(sm[:], 0.0), writes=[("sm",)])
                scr, ptr = RR(range(3)), RR(range(3))
                for qb in range(NQB):
                    s_src, tb = qb // NB, qb % NB
                    qs = qb % 2
                    ph.op(dq.next(), (lambda e, qs=qs, s_src=s_src, tb=tb: e.dma_start(
                        out=qT[qs][:],
                        in_=recv_mine(ph, e, RECV, s_src, cfg.pay["qa"], 256 * T).rearrange(
                            "o s (m p t) -> p (o s m) t", m=2, p=128)[:, :, tb * 512:(tb + 1) * 512])),
                        reads=[("mine", s_src, 0)], writes=[("qT", qs)], key=("qT", qs))
                    for m in range(2):
                        for j in range(NKT):
                            c_, p_ = scr.next(), ptr.next()
                            ph.op("tensor", (lambda e, c_=c_, m=m, j=j, qs=qs: e.matmul(
                                sc[c_][:], kT[:, m, j * 128:(j + 1) * 128], qT[qs][:, m, :], start=True, stop=True)),
                                reads=[("kT", j * 128 // T), ("qT", qs)], writes=[("sc", c_)])
                            delta = 512 * qb - 128 * j
                            if delta >= 128:
                                bias_fn, op1, n = (lambda: C["rl"][:]), ALU.subtract, delta // 128
                            elif delta <= -512:
                                bias_fn, op1, n = (lambda: C["rl"][:]), ALU.add, (-delta) // 128
                            else:
                                t = (-delta) // 128
                                bias_fn, op1, n = (lambda t=t: C["dg"][:, t * 512:(t + 1) * 512]), ALU.add, 0
                            ph.op("vector", (lambda e, c_=c_, bias_fn=bias_fn, op1=op1: e.scalar_tensor_tensor(
                                out=sb[c_][:], in0=sc[c_][:], scalar=scale, in1=bias_fn(), op0=ALU.mult, op1=op1)),
                                reads=[("sc", c_), ("c", "rl"), ("c", "dg")], writes=[("sb", c_)])
                            ph.op("scalar", (lambda e, c_=c_, p_=p_, n=n: e.activation(
                                out=pT[p_][:], in_=sb[c_][:], func=AF.Exp, bias=C["cb"][:, n:n + 1])),
                                reads=[("sb", c_), ("c", "cb")], writes=[("pT", p_)])
                            for qt in range(4):
                                ph.op("tensor", (lambda e, p_=p_, qt=qt, j=j: e.matmul(
                                    acc[qt][:, 0:257], pT[p_][:, qt * 128:(qt + 1) * 128], V[:, j, :], start=(j == 0), stop=(j == NKT - 1))),
                                    reads=[("pT", p_), ("V", j * 128 // T), ("Vone",)], writes=[("acc", qt)])
                        for qt in range(4):
                            ph.op("vector", (lambda e, qt=qt, m=m: e.reciprocal(out=sm[:, qt * 2 + m:qt * 2 + m + 1], in_=acc[qt][:, 256:257])),
                                  reads=[("acc", qt)], writes=[("sm", qt, m)])
                            if m == 0:
                                ph.op("vector", (lambda e, qt=qt: e.tensor_scalar(out=om[:, qt, 0, :], in0=acc[qt][:, 0:256],
                                                                                  scalar1=sm[:, qt * 2:qt * 2 + 1], scalar2=None, op0=ALU.mult)),
                                      reads=[("acc", qt), ("sm", qt, 0)], writes=[("om", qt, 0)])
                            else:
                                ph.op("vector", (lambda e, qt=qt: e.tensor_scalar(out=om[:, qt, 1, :], in0=acc[qt][:, 0:256],
                                                                                  scalar1=sm[:, qt * 2 + 1:qt * 2 + 2], scalar2=lt[:, 2:3],
                                                                                  op0=ALU.mult, op1=ALU.mult)),
                                      reads=[("acc", qt), ("sm", qt, 1), ("lt",)], writes=[("om", qt, 1)])
                    os_ = qb % 2
                    for qt in range(4):
                        ph.op("vector", (lambda e, qt=qt: e.tensor_tensor(out=om[:, qt, 0, :], in0=om[:, qt, 0, :], in1=om[:, qt, 1, :], op=ALU.add)),
                              reads=[("om", qt, 0), ("om", qt, 1)], writes=[("om", qt, 0)])
                        ph.op("vector", (lambda e, qt=qt: e.memset(sm[:, 8 + qt:9 + qt], 0.0)), writes=[("ssq", qt)])
                        ph.op("scalar", (lambda e, qt=qt: e.activation(out=junk[:], in_=om[:, qt, 0, :], func=AF.Square, accum_out=sm[:, 8 + qt:9 + qt])),
                              reads=[("om", qt, 0), ("ssq", qt)], writes=[("junk",), ("ssq", qt)])
                        rstd_ops(ph, sm[:, 8 + qt:9 + qt], sm[:, 8 + qt:9 + qt], 256, [("ssq", qt)], [("ssq", qt)])
                        ph.op("vector", (lambda e, qt=qt: e.scalar_tensor_tensor(out=obf[:, qt, :], in0=om[:, qt, 0, :], scalar=sm[:, 8 + qt:9 + qt],
                                                                                 in1=gsub[:], op0=ALU.mult, op1=ALU.mult)),
                              reads=[("om", qt, 0), ("ssq", qt), ("gsub",)], writes=[("obf", qt)])
                        for eh in range(2):
                            ph.op("tensor", (lambda e, qt=qt, eh=eh: e.transpose(out=ptp[:, (qt % 2 * 2 + eh) * 128:(qt % 2 * 2 + eh + 1) * 128],
                                                                                 in_=obf[:, qt, eh * 128:(eh + 1) * 128], identity=C["ident"][:])),
                                  reads=[("obf", qt), ("c", "ident")], writes=[("ptp",)])
                            ph.op("scalar", (lambda e, qt=qt, eh=eh, os_=os_: e.activation(
                                out=oTs[os_][:, eh, qt * 128:(qt + 1) * 128], in_=ptp[:, (qt % 2 * 2 + eh) * 128:(qt % 2 * 2 + eh + 1) * 128], func=AF.Copy)),
                                reads=[("ptp",)], writes=[("oTs", os_, qt, eh)])
                    ph.op(dq.next(), (lambda e, os_=os_, s_src=s_src, tb=tb: e.dma_start(
                        out=SENDB[s_src:s_src + 1, 0:256 * T].rearrange("o (m p t) -> p (o m) t", m=2, p=128)[:, :, tb * 512:(tb + 1) * 512],
                        in_=oTs[os_][:])),
                        reads=[("oTs", os_, qt, eh) for qt in range(4) for eh in range(2)], writes=[], key=("oTs", os_))
                ph.emit()

        def phase_attn_bc(l, which):
            ph = Phase(nc, sems, which + "%d" % l)
            isb = which == "b"
            scale = (192 ** -0.5) if isb else (HD ** -0.5)
            with ExitStack() as st:
                C = load_consts(ph, st, ["ident"] + ([] if isb else ["rowsel", "emat"]))
                kT = st.enter_context(sbt("kT", [128, S], BF16))
                kR = st.enter_context(sbt("kR", [64, S], BF16)) if isb else None
                V = st.enter_context(sbt("V", [128, NKT, 129], BF16))
                qT = [st.enter_context(sbt("qT%d" % i, [128, 512], BF16)) for i in range(2)]
                qR = [st.enter_context(sbt("qR%d" % i, [64, 512], BF16)) for i in range(2)] if isb else None
                tbl = st.enter_context(sbt("tbl", [128, 22 * 64], F32)) if not isb else None
                sb = [st.enter_context(sbt("sb%d" % i, [128, 512], F32)) for i in range(3)] if not isb else None
                pT = [st.enter_context(sbt("pT%d" % i, [128, 512], BF16)) for i in range(3)]
                obf = st.enter_context(sbt("obf", [128, 4, 128], BF16))
                oTs = [st.enter_context(sbt("oTs%d" % i, [128, 512], BF16)) for i in range(2)]
                sm = st.enter_context(sbt("sm", [128, 8], F32))
                acc = [st.enter_context(pst("acc%d" % i, [128, 512], F32)) for i in range(4)]
                sc = [st.enter_context(pst("sc%d" % i, [128, 512], F32)) for i in range(3)]
                ptp = st.enter_context(pst("ptp", [128, 512], BF16))
                dq = RR(DQ)
                kname, vname, qname = ("kbn", "vb", "qbn") if isb else ("kc", "vc", "qc")
                ph.op("vector", lambda e: e.memset(V[:, :, 128:129], 1.0), writes=[("Vone",)])
                if isb:
                    for s in range(NR):
                        ph.op(dq.next(), (lambda e, s=s: e.dma_start(
                            out=kR[:, s * T:(s + 1) * T],
                            in_=recv_mine(ph, e, RECV, s, cfg.pay["kr"], 64 * T).rearrange("o s (p t) -> p (o s t)", p=64))),
                            writes=[("kR", s)], key=("kR", s))
                scr, ptr = RR(range(3)), RR(range(3))
                for hh in range(2):
                    for s in range(NR):
                        ph.op(dq.next(), (lambda e, s=s, hh=hh: e.dma_start(
                            out=kT[:, s * T:(s + 1) * T],
                            in_=recv_mine(ph, e, RECV, s, cfg.pay[kname] + hh * 128 * T, 128 * T).rearrange("o s (p t) -> p (o s t)", p=128))),
                            writes=[("kT", s)], key=("kT", s))
                        ph.op(dq.next(), (lambda e, s=s, hh=hh: e.dma_start(
                            out=V[:, s * (T // 128):(s + 1) * (T // 128), 0:128],
                            in_=recv_mine(ph, e, RECV, s, cfg.pay[vname], 256 * T).rearrange(
                                "o s (j p e) -> p (o s j) e", p=128, e=256)[:, :, hh * 128:(hh + 1) * 128])),
                            writes=[("V", s)], key=("V", s))
                    if not isb:
                        ph.op("sync", (lambda e, hh=hh: e.dma_start(out=tbl[:], in_=tb_in[l * 2 + hh, :, :])), writes=[("tbl",)], key=("tbl",))
                    for qb in range(NQB):
                        s_src, tb = qb // NB, qb % NB
                        qs = qb % 2
                        ph.op(dq.next(), (lambda e, qs=qs, s_src=s_src, tb=tb, hh=hh: e.dma_start(
                            out=qT[qs][:],
                            in_=recv_mine(ph, e, RECV, s_src, cfg.pay[qname] + hh * 128 * T, 128 * T).rearrange(
                                "o s (p t) -> p (o s t)", p=128)[:, tb * 512:(tb + 1) * 512])),
                            writes=[("qT", qs)], key=("qT", qs))
                        if isb:
                            ph.op(dq.next(), (lambda e, qs=qs, s_src=s_src, tb=tb, hh=hh: e.dma_start(
                                out=qR[qs][:],
                                in_=recv_mine(ph, e, RECV, s_src, cfg.pay["qbr"] + hh * 64 * T, 64 * T).rearrange(
                                    "o s (p t) -> p (o s t)", p=64)[:, tb * 512:(tb + 1) * 512])),
                                writes=[("qR", qs)], key=("qR", qs))
                            tiles = [(j, None) for j in range(NKT)]
                        else:
                            tiles = [(4 * qb - 2 + jj, jj) for jj in range(8) if 0 <= 4 * qb - 2 + jj < NKT]
                        for ti, (j, jj) in enumerate(tiles):
                            c_, p_ = scr.next(), ptr.next()
                            first, last = ti == 0, ti == len(tiles) - 1
                            ph.op("tensor", (lambda e, c_=c_, j=j, qs=qs: e.matmul(
                                sc[c_][:], kT[:, j * 128:(j + 1) * 128], qT[qs][:], start=True, stop=False)),
                                reads=[("kT", j * 128 // T), ("qT", qs)], writes=[("sc", c_)])
                            if isb:
                                ph.op("tensor", (lambda e, c_=c_, j=j, qs=qs: e.matmul(
                                    sc[c_][:], kR[0:64, j * 128:(j + 1) * 128], qR[qs][0:64, :], start=False, stop=True)),
                                    reads=[("kR", j * 128 // T), ("qR", qs)], writes=[("sc", c_)])
                                ph.op("scalar", (lambda e, c_=c_, p_=p_: e.activation(out=pT[p_][:], in_=sc[c_][:], func=AF.Exp, scale=scale)),
                                      reads=[("sc", c_)], writes=[("pT", p_)])
                            else:
                                cls = emat_cls[qb]
                                ph.op("tensor", (lambda e, c_=c_, jj=jj, cls=cls: e.matmul(
                                    sc[c_][:], C["rowsel"][0:16, jj * 128:(jj + 1) * 128], C["emat"][0:16, cls * 512:(cls + 1) * 512],
                                    start=False, stop=True)),
                                    reads=[("c", "rowsel"), ("c", "emat")], writes=[("sc", c_)])
                                ph.op("vector", (lambda e, c_=c_, jj=jj: e.scalar_tensor_tensor(
                                    out=sb[c_][:], in0=sc[c_][:], scalar=scale, in1=tbl[:, (14 - 2 * jj) * 64:(14 - 2 * jj) * 64 + 512],
                                    op0=ALU.mult, op1=ALU.add)),
                                    reads=[("sc", c_), ("tbl",)], writes=[("sb", c_)])
                                ph.op("scalar", (lambda e, c_=c_, p_=p_: e.activation(out=pT[p_][:], in_=sb[c_][:], func=AF.Exp)),
                                      reads=[("sb", c_)], writes=[("pT", p_)])
                            for qt in range(4):
                                ph.op("tensor", (lambda e, p_=p_, qt=qt, j=j, first=first, last=last: e.matmul(
                                    acc[qt][:, 0:129], pT[p_][:, qt * 128:(qt + 1) * 128], V[:, j, :], start=first, stop=last)),
                                    reads=[("pT", p_), ("V", j * 128 // T), ("Vone",)], writes=[("acc", qt)])
                        os_ = qb % 2
                        for qt in range(4):
                            ph.op("vector", (lambda e, qt=qt: e.reciprocal(out=sm[:, qt:qt + 1], in_=acc[qt][:, 128:129])),
                                  reads=[("acc", qt)], writes=[("sm", qt)])
                            ph.op("vector", (lambda e, qt=qt: e.tensor_scalar(out=obf[:, qt, :], in0=acc[qt][:, 0:128], scalar1=sm[:, qt:qt + 1],
                                                                              scalar2=None, op0=ALU.mult)),
                                  reads=[("acc", qt), ("sm", qt)], writes=[("obf", qt)])
                            ph.op("tensor", (lambda e, qt=qt: e.transpose(out=ptp[:, qt * 128:(qt + 1) * 128], in_=obf[:, qt, :], identity=C["ident"][:])),
                                  reads=[("obf", qt), ("c", "ident")], writes=[("ptp",)])
                            ph.op("scalar", (lambda e, qt=qt, os_=os_: e.activation(out=oTs[os_][:, qt * 128:(qt + 1) * 128],
                                                                                    in_=ptp[:, qt * 128:(qt + 1) * 128], func=AF.Copy)),
                                  reads=[("ptp",)], writes=[("oTs", os_, qt)])
                        boff = (256 if isb else 512) * T + hh * 128 * T
                        ph.op(dq.next(), (lambda e, os_=os_, s_src=s_src, tb=tb, boff=boff: e.dma_start(
                            out=SENDB[s_src:s_src + 1, boff:boff + 128 * T].rearrange("o (p t) -> p (o t)", p=128)[:, tb * 512:(tb + 1) * 512],
                            in_=oTs[os_][:])),
                            reads=[("oTs", os_, qt) for qt in range(4)], writes=[], key=("oTs", os_))
                ph.emit()

        def phase_merge(l):
            ph = Phase(nc, sems, "m%d" % l)
            WB = [wl("wba", l, D), wl("wbb", l, D), wl("wbc", l, D)]
            with ExitStack() as st:
                yg = st.enter_context(sbt("yg", [128, 48, 512], BF16))
                ys = [st.enter_context(sbt("ys%d" % i, [128, 6, 512], BF16)) for i in range(2)]
                gs = [st.enter_context(sbt("gs%d" % i, [128, 6, 512], BF16)) for i in range(2)]
                wb = [[st.enter_context(sbt("wb%d_%d" % (i, b), [128, 16, 256], BF16)) for b in range(3)] for i in range(2)]
                sg = [st.enter_context(sbt("sg%d" % i, [128, 3, 512], BF16)) for i in range(2)]
                m1 = st.enter_context(sbt("m1", [128, 512], F32))
                m2 = st.enter_context(sbt("m2", [128, 512], F32))
                ms = [st.enter_context(sbt("ms%d" % i, [128, 512], BF16)) for i in range(2)]
                ps = [st.enter_context(pst("ps%d" % i, [128, 512], F32)) for i in range(6)]
                dq = RR(DQ)
                gather_ops(ph, [(SENDB, RECVB, 0, PAYB)], MINEB, "b")
                psr = RR(range(6))
                wi = 0
                oi = 0
                for tb in range(NB):
                    tcol = tb * 512
                    for d in range(NR):
                        sl = d % 2
                        ph.op(dq.next(), (lambda e, d=d, sl=sl, tcol=tcol: e.dma_start(
                            out=ys[sl][:],
                            in_=recv_mine(ph, e, RECVB, d, 0, 768 * T).rearrange("o s (c p t) -> p (o s c) t", c=6, p=128)[:, :, tcol:tcol + 512])),
                            reads=[("mine", d, 0)], writes=[("ys", sl)], key=("ys", sl))
                        ph.op(dq.next(), (lambda e, d=d, sl=sl, tcol=tcol: e.dma_start(
                            out=gs[sl][:], in_=GATES[d * 6:(d + 1) * 6, :, tcol:tcol + 512].rearrange("c p t -> p c t"))),
                            writes=[("gs", sl)], key=("gs", sl))
                        ph.op("vector", (lambda e, d=d, sl=sl: e.tensor_tensor(out=yg[:, d * 6:(d + 1) * 6, :], in0=ys[sl][:], in1=gs[sl][:], op=ALU.mult)),
                              reads=[("ys", sl), ("gs", sl)], writes=[("yg", d)])
                    for oc2 in range(D // 256):
                        ws = wi % 2
                        wi += 1
                        for b in range(3):
                            ph.op(dq.next(), (lambda e, ws=ws, b=b, oc2=oc2: e.dma_start(
                                out=wb[ws][b][:], in_=WB[b][:, oc2 * 256:(oc2 + 1) * 256].rearrange("(k p) c -> p k c", p=128))),
                                writes=[("wb", ws, b)], key=("wb", ws, b))
                        for half in range(2):
                            oc = oc2 * 2 + half
                            os_ = oi % 2
                            oi += 1
                            ph.op(dq.next(), (lambda e, os_=os_, oc=oc, tcol=tcol: e.dma_start(
                                out=sg[os_][:], in_=SIG[oc, :, :, tcol:tcol + 512].rearrange("b p t -> p b t"))),
                                writes=[("sg", os_)], key=("sg", os_))
                            pp = [psr.next() for _ in range(3)]
                            for b in range(3):
                                for kx in range(16):
                                    idx = (kx // 2) * 6 + b * 2 + kx % 2
                                    ph.op("tensor", (lambda e, p_=pp[b], b=b, kx=kx, idx=idx, ws=ws, half=half: e.matmul(
                                        ps[p_][:], wb[ws][b][:, kx, half * 128:(half + 1) * 128], yg[:, idx, :], start=(kx == 0), stop=(kx == 15))),
                                        reads=[("wb", ws, b), ("yg", kx // 2)], writes=[("ps", pp[b])])
                            ph.op("vector", (lambda e, p_=pp[0], os_=os_: e.tensor_tensor(out=m1[:], in0=ps[p_][:], in1=sg[os_][:, 0, :], op=ALU.mult)),
                                  reads=[("ps", pp[0]), ("sg", os_)], writes=[("m1",)])
                            ph.op("vector", (lambda e, p_=pp[1], os_=os_: e.tensor_tensor(out=m2[:], in0=ps[p_][:], in1=sg[os_][:, 1, :], op=ALU.mult)),
                                  reads=[("ps", pp[1]), ("sg", os_)], writes=[("m2",)])
                            ph.op("vector", (lambda e: e.tensor_tensor(out=m1[:], in0=m1[:], in1=m2[:], op=ALU.add)),
                                  reads=[("m1",), ("m2",)], writes=[("m1",)])
                            ph.op("vector", (lambda e, p_=pp[2], os_=os_: e.tensor_tensor(out=m2[:], in0=ps[p_][:], in1=sg[os_][:, 2, :], op=ALU.mult)),
                                  reads=[("ps", pp[2]), ("sg", os_)], writes=[("m2",)])
                            ph.op("vector", (lambda e, os_=os_: e.tensor_tensor(out=ms[os_][:], in0=m1[:], in1=m2[:], op=ALU.add)),
                                  reads=[("m1",), ("m2",)], writes=[("ms", os_)])
                            ph.op(dq.next(), (lambda e, os_=os_, oc=oc, tcol=tcol: e.dma_start(out=MT[oc, :, tcol:tcol + 512], in_=ms[os_][:])),
                                  reads=[("ms", os_)], writes=[], key=("ms", os_))
                ph.emit()

        def phase_oproj(l, xsrc):
            ph = Phase(nc, sems, "o%d" % l)
            W = wl("wo", l, D)
            with ExitStack() as st:
                mT = st.enter_context(sbt("mT", [128, KC, TH], BF16))
                wt = [st.enter_context(sbt("wt%d" % i, [128, KC, 512], BF16)) for i in range(2)]
                xt = [st.enter_context(sbt("xt%d" % i, [128, 512], F32)) for i in range(4)]
                ps = [st.enter_context(pst("ps%d" % i, [128, 512], F32)) for i in range(6)]
                dq = RR(DQ)
                psr, xr = RR(range(6)), RR(range(4))
                gi = 0
                NGc = (D + 511) // 512
                for hh in range(NH):
                    tok0 = hh * TH
                    for k4 in range(0, KC, KG):
                        ph.op(dq.next(), (lambda e, k4=k4, tok0=tok0: e.dma_start(
                            out=mT[:, k4:k4 + KG, :], in_=MT[k4:k4 + KG, :, tok0:tok0 + TH].rearrange("k p t -> p k t"))),
                            writes=[("mT", k4 // KG)], key=("mT", k4 // KG))
                    for g in range(NGc):
                        c0 = g * 512
                        ncol = min(512, D - c0)
                        ws = gi % 2
                        gi += 1
                        ph.op(dq.next(), (lambda e, ws=ws, c0=c0, ncol=ncol: e.dma_start(
                            out=wt[ws][:, :, 0:ncol], in_=W[:, c0:c0 + ncol].rearrange("(k p) c -> p k c", p=128))),
                            writes=[("wt", ws)], key=("wt", ws))
                        for tt in range(TH // 128):
                            r0 = tok0 + tt * 128
                            x_ = xr.next()
                            ph.op(dq.next(), (lambda e, x_=x_, r0=r0, c0=c0, ncol=ncol: e.dma_start(out=xt[x_][:, 0:ncol], in_=xsrc[r0:r0 + 128, c0:c0 + ncol])),
                                  writes=[("xt", x_)], key=("xt", x_))
                            p_ = psr.next()
                            for kc in range(KC):
                                ph.op("tensor", (lambda e, p_=p_, kc=kc, tt=tt, ws=ws, ncol=ncol: e.matmul(
                                    ps[p_][:, 0:ncol], mT[:, kc, tt * 128:(tt + 1) * 128], wt[ws][:, kc, 0:ncol], start=(kc == 0), stop=(kc == KC - 1))),
                                    reads=[("mT", kc // KG), ("wt", ws)], writes=[("ps", p_)])
                            ph.op("vector", (lambda e, p_=p_, x_=x_, ncol=ncol: e.tensor_tensor(out=xt[x_][:, 0:ncol], in0=ps[p_][:, 0:ncol], in1=xt[x_][:, 0:ncol], op=ALU.add)),
                                  reads=[("ps", p_), ("xt", x_)], writes=[("xt", x_)])
                            ph.op(dq.next(), (lambda e, x_=x_, r0=r0, c0=c0, ncol=ncol: e.dma_start(out=XRES[r0:r0 + 128, c0:c0 + ncol], in_=xt[x_][:, 0:ncol])),
                                  reads=[("xt", x_)], writes=[], key=("xt", x_))
                ph.emit()

        def phase_final():
            ph = Phase(nc, sems, "f")
            with ExitStack() as st:
                C = load_consts(ph, st, ["g_fin"])
                xt = [st.enter_context(sbt("xt%d" % i, [128, D], F32)) for i in range(2)]
                yt = [st.enter_context(sbt("yt%d" % i, [128, D], F32)) for i in range(2)]
                ss = st.enter_context(sbt("ss", [128, T // 128], F32))
                ph.op("vector", lambda e: e.memset(ss[:], 0.0), writes=[("ss", tt) for tt in range(T // 128)])
                for tt in range(T // 128):
                    sl = tt % 2
                    ph.op("sync", (lambda e, sl=sl, tt=tt: e.dma_start(out=xt[sl][:], in_=XRES[tt * 128:(tt + 1) * 128, :])),
                          writes=[("xt", sl)], key=("xt", sl))
                    ph.op("scalar", (lambda e, sl=sl, tt=tt: e.activation(out=yt[sl][:], in_=xt[sl][:], func=AF.Square, accum_out=ss[:, tt:tt + 1])),
                          reads=[("xt", sl)], writes=[("yt", sl), ("ss", tt)])
                    rstd_ops(ph, ss[:, tt:tt + 1], ss[:, tt:tt + 1], D, [("ss", tt)], [("ss", tt)])
                    ph.op("vector", (lambda e, sl=sl, tt=tt: e.scalar_tensor_tensor(out=yt[sl][:], in0=xt[sl][:], scalar=ss[:, tt:tt + 1], in1=C["g_fin"][:],
                                                                                    op0=ALU.mult, op1=ALU.mult)),
                          reads=[("xt", sl), ("ss", tt), ("c", "g_fin")], writes=[("yt", sl)])
                    ph.op(DQ[-1], (lambda e, sl=sl, tt=tt: e.dma_start(out=out_ap[tt * 128:(tt + 1) * 128, :], in_=yt[sl][:])),
                          reads=[("yt", sl)], writes=[], key=("yt", sl))
                ph.emit()

        ph0 = Phase(nc, sems, "w")
        weight_ops(ph0)
        if ph0.ops:
            ph0.emit()
        xsrc = x_in
        lim = getattr(cfg, "max_phases", 10 ** 9)
        cnt = [0]

        def run(fn, *a):
            cnt[0] += 1
            if cnt[0] <= lim and cnt[0] not in getattr(cfg, "skip", ()):
                fn(*a)
        for l in range(L):
            run(phase_norm, l, xsrc)
            for _ in range(getattr(cfg, "rep_norm", 0)):
                phase_norm(l, xsrc)
            run(phase_inproj, l)
            run(phase_bprep, l)
            run(phase_attn_a, l)
            run(phase_attn_bc, l, "b")
            run(phase_attn_bc, l, "c")
            run(phase_merge, l)
            run(phase_oproj, l, xsrc)
            xsrc = XRES
        phase_final()
    return nc


def _bf(a):
    return np.asarray(a, dtype=np.float32).astype(ml_dtypes.bfloat16)


def make_in_maps(cfg, inp):
    D, T, L, QL, KVL, S = cfg.D, cfg.T, cfg.L, cfg.QL, cfg.KVL, cfg.S
    f32 = np.float32
    splits = [2048, 2048, 2048, 2048, QL, KVL, 64, 2048, 2048, 2048, 2048, 2048, D, D, D]
    names = ["qa", "ka", "va", "ga", "cq", "ckv", "kr", "gb", "qc", "kc", "vc", "gc", "sa", "sb", "sc"]
    starts = np.concatenate([[0], np.cumsum(splits)])
    rng = {n: np.arange(starts[i], starts[i + 1]) for i, n in enumerate(names)}
    kr = rng["kr"]
    perm = np.concatenate([rng[n] for n in ("qa", "ka", "va", "ga", "cq", "ckv", "gb", "qc", "kc", "vc", "gc", "sa", "sb", "sc")]
                          + [kr, kr[32:], kr[:32]])
    assert perm.size == cfg.NCOLX
    w_in = np.asarray(inp["w_in"], dtype=f32)

    def shard_rows(w, r):
        rows = w.shape[1] // NR
        return np.ascontiguousarray(w[:, r * rows:(r + 1) * rows, :].transpose(1, 0, 2)).reshape(rows, -1)

    w_uq = np.asarray(inp["b_w_uq"], dtype=f32).reshape(L, QL, 16, 192)
    w_uq_ext = np.concatenate([w_uq[..., 0:192], w_uq[..., 160:192], w_uq[..., 128:160]], axis=-1).reshape(L, QL, 4096)
    w_ukv = np.asarray(inp["b_w_ukv"], dtype=f32).reshape(L, KVL, 16, 256)
    w_ukv_r = np.concatenate([w_ukv[..., 0:128].reshape(L, KVL, 2048), w_ukv[..., 128:256].reshape(L, KVL, 2048)], axis=-1)

    def tcols(g, nch):
        return np.ascontiguousarray(np.asarray(g, dtype=f32).reshape(L, nch, 128).transpose(2, 0, 1)).reshape(128, L * nch)

    g_in = tcols(inp["norm_g"], cfg.KC)
    g_q = tcols(inp["b_q_norm_g"], cfg.QC)
    g_kv = tcols(inp["b_kv_norm_g"], cfg.KVC)
    g_sub = np.ascontiguousarray(np.broadcast_to(np.asarray(inp["a_subln_g"], dtype=f32).reshape(1, L * 256), (128, L * 256)))
    lam = np.stack([np.asarray(inp[k], dtype=f32) for k in ("a_lam_q1", "a_lam_k1", "a_lam_q2", "a_lam_k2")], axis=1)
    lamv = np.ascontiguousarray(np.broadcast_to(lam.reshape(1, L * 512), (128, L * 512)))
    g_fin = np.ascontiguousarray(np.broadcast_to(np.asarray(inp["final_norm_g"], dtype=f32).reshape(1, D), (128, D)))
    ident = _bf(np.eye(128))
    ones = _bf(np.ones((128, 128)))
    rowsel = np.zeros((16, 8, 128), f32)
    for jj in range(8):
        for k in range(128):
            rowsel[2 * jj + k // 64, jj, k] = 1.0
    rowsel = _bf(rowsel.reshape(16, 8 * 128))
    emat = _bf(cfg.emat_np)
    inv_freq = np.power(f32(10000.0), -(np.arange(0, 64, 2, dtype=f32) / f32(64))).astype(f32)
    rpb = np.asarray(inp["c_rpb"], dtype=f32)
    kc_i = np.arange(64)[:, None]
    qc_i = np.arange(64)[None, :]
    cs_q = np.clip(qc_i - 8, 0, 48)
    inwin = (kc_i >= cs_q) & (kc_i <= cs_q + 15)
    dc = np.clip(kc_i - qc_i + 15, 0, 30)
    tb_all = np.full((L, 16, 128, 22, 64), NEG, f32)
    for rk in range(2):
        for tidx in range(22):
            dr = (10 if rk == 0 else 11) - tidx
            if abs(dr) > 7:
                continue
            vals = rpb[:, :, dr + 7, :][:, :, dc]
            tb_all[:, :, rk * 64:(rk + 1) * 64, tidx, :] = np.where(inwin[None, None], vals, f32(NEG))
    x = np.asarray(inp["x"], dtype=f32).reshape(S, D)
    kq = (np.arange(512)[None, :] - np.arange(128)[:, None]).astype(f32)
    in_maps = []
    w_in_p = w_in[:, :, perm]
    wba, wbb, wbc, wo = (np.asarray(inp[k], dtype=f32) for k in ("w_br_a", "w_br_b", "w_br_c", "w_o"))
    for r in range(NR):
        slope = f32(2.0 ** (-(r + 1)))
        pos = (np.arange(T, dtype=f32) + f32(r * T))
        ang = (pos[:, None] * inv_freq[None, :]).astype(f32)
        cos, sin = np.cos(ang).astype(f32).T, np.sin(ang).astype(f32).T
        m = {
            "x": np.ascontiguousarray(x[r * T:(r + 1) * T]),
            "w_uq_sh": shard_rows(w_uq_ext, r),
            "w_ukv_sh": shard_rows(w_ukv_r, r),
            "w_br_a_sh": shard_rows(wba, r), "w_br_b_sh": shard_rows(wbb, r), "w_br_c_sh": shard_rows(wbc, r),
            "w_o_sh": shard_rows(wo, r),
            "g_in": g_in, "g_q": g_q, "g_kv": g_kv, "g_sub": g_sub, "lamv": lamv, "g_fin": g_fin,
            "cs1": np.ascontiguousarray(np.concatenate([cos, cos], 0)),
            "cs2": np.ascontiguousarray(np.concatenate([-sin, sin], 0)),
            "tb_c": np.ascontiguousarray(tb_all[:, 2 * r:2 * r + 2]).reshape(L * 2, 128, 22 * 64),
            "rl_a": np.ascontiguousarray(slope * kq),
            "dg_a": np.ascontiguousarray(np.concatenate([-slope * np.abs(kq - f32(128 * t)) for t in range(4)], axis=1)),
            "cb_a": np.ascontiguousarray(np.broadcast_to((-slope * 128.0 * np.arange(cfg.NKT + 8, dtype=f32))[None, :], (128, cfg.NKT + 8))),
            "ident": ident, "ones": ones, "rowsel": rowsel, "emat": emat,
        }
        rows_ = D // NR
        CS = (cfg.NCH // 8) * 4
        for l_ in range(L):
            for pi, (a_, b_) in enumerate(((0, CS), (CS, cfg.NCH))):
                m["w_in_sh%d_%d" % (l_, pi)] = np.ascontiguousarray(w_in_p[l_, r * rows_:(r + 1) * rows_, a_ * 128:b_ * 128])
        in_maps.append(m)
    return in_maps


_NC_CACHE = {}


def run_cfg(cfg, inp):
    key = (cfg.D, cfg.T, cfg.L, cfg.QL, cfg.KVL)
    if key not in _NC_CACHE:
        _NC_CACHE[key] = build_program(cfg)
    nc = _NC_CACHE[key]
    in_maps = make_in_maps(cfg, inp)
    res = run_bass_kernel_spmd(nc, in_maps, core_ids=list(range(NR)))
    cfg.last_results = res.results
    out = np.concatenate([np.asarray(r["out"], dtype=np.float32) for r in res.results], axis=0)
    return out.reshape(1, cfg.S, cfg.D)


def kernel(**inputs):
    cfg = Cfg()
    return run_cfg(cfg, inputs)
```
